# Optimizing a Trainium2 kernel written in Bass

```python
import math
import jax
import jax.numpy as jnp
from jax import lax
import numpy as np

D_MODEL = 1024
BATCH = 2
SEQ = 16384
DEPTH = 4

N_MIXERS = 2
N_A_LAYERS = (DEPTH + 1) // 2
N_B_LAYERS = DEPTH // 2
NORM_EPS = 1e-5
D_FF = 4 * D_MODEL
N_MOD = 6

S5_WIDTH = D_MODEL
S5_GROUP = 16
S5_GROUPS = S5_WIDTH // S5_GROUP
S5_STATE = 64
S5_CHUNK = 128
S5_DT_MIN = 0.001
S5_DT_MAX = 0.1

M2_D_INNER = 2 * D_MODEL
M2_HEADDIM = 64
M2_HEADS = M2_D_INNER // M2_HEADDIM
M2_GROUPS = 4
M2_HPG = M2_HEADS // M2_GROUPS
M2_STATE = 128
M2_CONV = 4
M2_CHUNK = 128
M2_CONV_DIM = M2_D_INNER + 2 * M2_GROUPS * M2_STATE
M2_IN_DIM = M2_D_INNER + M2_CONV_DIM + M2_HEADS
M2_DT_MIN = 0.001
M2_DT_MAX = 0.1

kernel_name = 'hybrid_s5_ssd_adaln_trunk'


def _rmsnorm(x, g):
    xf = x.astype(jnp.float32)
    y = xf * lax.rsqrt(jnp.mean(xf * xf, axis=-1, keepdims=True) + NORM_EPS)
    return (y * g.astype(jnp.float32)).astype(x.dtype)


def _modulate(h, shift, scale):
    return h * (1 + scale) + shift


def _sq_relu_mlp(h, w1, w2):
    a = jax.nn.relu(h @ w1)
    return (a * a) @ w2


def _s5_combine(e1, e2):
    a1, b1 = e1
    a2, b2 = e2
    return a2 * a1, a2 * b1 + b2


def _s5_mixer(h, w_in, lam_re, lam_im, log_dt, b_re, b_im, c_re, c_im, d, w_glu, b_glu):
    f32 = jnp.float32
    bsz, L, _ = h.shape
    nc = L // S5_CHUNK
    u = (h @ w_in).astype(f32)
    lam = lax.complex(lam_re.astype(f32), lam_im.astype(f32))
    dt = jnp.exp(log_dt.astype(f32))[:, None]
    lam_dt = lam * dt
    lam_bar = jnp.exp(lam_dt)
    b_bar = ((lam_bar - 1) / lam)[..., None] * lax.complex(b_re.astype(f32), b_im.astype(f32))
    c_mat = lax.complex(c_re.astype(f32), c_im.astype(f32))
    steps = jnp.arange(1, S5_CHUNK + 1, dtype=f32)[:, None, None]
    a_pow = jnp.exp(lam_dt[None] * steps)
    a_elems = jnp.broadcast_to(lam_bar, (bsz, S5_CHUNK, S5_GROUPS, S5_STATE))
    u_blocks = u.reshape(bsz, nc, S5_CHUNK, S5_GROUPS, S5_GROUP).transpose(1, 0, 2, 3, 4)

    def block_step(carry, u_blk):
        bu = jnp.einsum('btgh,gph->btgp', u_blk.astype(jnp.complex64), b_bar)
        _, hs = lax.associative_scan(_s5_combine, (a_elems, bu), axis=1)
        hs = hs + a_pow[None] * carry[:, None]
        y = jnp.einsum('btgp,ghp->btgh', hs, c_mat).real
        return hs[:, -1], y

    carry0 = jnp.zeros((bsz, S5_GROUPS, S5_STATE), jnp.complex64)
    _, ys = lax.scan(block_step, carry0, u_blocks)
    y = ys.transpose(1, 0, 2, 3, 4).reshape(bsz, L, S5_WIDTH) + d.astype(f32) * u
    g = jax.nn.gelu(y)
    ab = g @ w_glu.astype(f32) + b_glu.astype(f32)
    val, gate = jnp.split(ab, 2, axis=-1)
    return (val * jax.nn.sigmoid(gate)).astype(h.dtype)


def _causal_dwconv(u, w):
    return lax.conv_general_dilated(u, w[:, None, :].astype(u.dtype), (1,), [(M2_CONV - 1, 0)],
                                    dimension_numbers=('NWC', 'WIO', 'NWC'),
                                    feature_group_count=u.shape[-1])


def _segsum(a):
    t = a.shape[-1]
    cs = jnp.cumsum(a, axis=-1)
    diff = cs[..., :, None] - cs[..., None, :]
    mask = jnp.tril(jnp.ones((t, t), dtype=bool))
    return jnp.where(mask, diff, -jnp.inf)


def _ssd_mixer(h, w_in, conv_w, conv_b, dt_bias, a_log, d, norm_g, w_out):
    f32 = jnp.float32
    bsz, L, _ = h.shape
    nc = L // M2_CHUNK
    G, R, P, N, Q = M2_GROUPS, M2_HPG, M2_HEADDIM, M2_STATE, M2_CHUNK
    zxbcdt = h @ w_in
    z, xbc, dt_raw = jnp.split(zxbcdt, [M2_D_INNER, M2_D_INNER + M2_CONV_DIM], axis=-1)
    xbc = jax.nn.silu(_causal_dwconv(xbc, conv_w) + conv_b)
    xs, b_in, c_in = jnp.split(xbc, [M2_D_INNER, M2_D_INNER + G * N], axis=-1)
    dt = jax.nn.softplus(dt_raw.astype(f32) + dt_bias.astype(f32))
    a = -jnp.exp(a_log.astype(f32)).reshape(G, R)
    xs = xs.astype(f32).reshape(bsz, nc, Q, G, R, P)
    b_in = b_in.astype(f32).reshape(bsz, nc, Q, G, N)
    c_in = c_in.astype(f32).reshape(bsz, nc, Q, G, N)
    dt = dt.reshape(bsz, nc, Q, G, R)
    xdt = xs * dt[..., None]
    a_dt = jnp.transpose(dt * a, (0, 3, 4, 1, 2))
    a_cs = jnp.cumsum(a_dt, axis=-1)
    l_mat = jnp.exp(_segsum(a_dt))
    cb = jnp.einsum('bclgn,bcsgn->bcgls', c_in, b_in)
    y_diag = jnp.einsum('bcgls,bgrcls,bcsgrp->bclgrp', cb, l_mat, xdt)
    decay_states = jnp.exp(a_cs[..., -1:] - a_cs)
    states = jnp.einsum('bclgn,bgrcl,bclgrp->bcgrpn', b_in, decay_states, xdt)
    chunk_decay = jnp.exp(a_cs[..., -1])

    def carry_step(carry, inp):
        st, dec = inp
        return dec[..., None, None] * carry + st, carry

    _, prev = lax.scan(carry_step, jnp.zeros((bsz, G, R, P, N), f32),
                       (jnp.moveaxis(states, 1, 0), jnp.moveaxis(chunk_decay, -1, 0)))
    prev = jnp.moveaxis(prev, 0, 1)
    y_off = jnp.einsum('bclgn,bcgrpn,bgrcl->bclgrp', c_in, prev, jnp.exp(a_cs))
    y = y_diag + y_off + xs * d.astype(f32).reshape(G, R, 1)
    y = y.reshape(bsz, L, M2_D_INNER) * jax.nn.silu(z.astype(f32))
    yg = y.reshape(bsz, L, G, M2_D_INNER // G)
    yg = yg * lax.rsqrt(jnp.mean(yg * yg, axis=-1, keepdims=True) + NORM_EPS)
    y = yg.reshape(bsz, L, M2_D_INNER) * norm_g.astype(f32)
    return y.astype(h.dtype) @ w_out


def setup_inputs(seed: int = 0) -> dict:
    f32 = jnp.float32
    key = jax.random.key(seed)
    it = iter(jax.random.split(key, 28))

    def nrm(shape, scale):
        return scale * jax.random.normal(next(it), shape, f32)

    D, F = D_MODEL, D_FF
    NA, NB = N_A_LAYERS, N_B_LAYERS
    G, P, H = S5_GROUPS, S5_STATE, S5_GROUP
    x = nrm((BATCH, SEQ, D), 1.0)
    c = nrm((BATCH, D), 1.0)
    ada_w = nrm((DEPTH, D, N_MOD * D), 0.5 * D ** -0.5)
    ada_b = nrm((DEPTH, N_MOD * D), 0.02)
    norm_mix_g = 1.0 + nrm((DEPTH, D), 0.05)
    norm_mlp_g = 1.0 + nrm((DEPTH, D), 0.05)
    mlp_w1 = nrm((DEPTH, D, F), D ** -0.5)
    mlp_w2 = nrm((DEPTH, F, D), F ** -0.5)
    s5_w_in = nrm((NA, D, S5_WIDTH), D ** -0.5)
    s5_lambda_re = -0.5 + nrm((NA, G, P), 0.01)
    s5_lambda_im = math.pi * jnp.arange(P, dtype=f32) + nrm((NA, G, P), 0.01)
    s5_log_dt = jax.random.uniform(next(it), (NA, G), f32, minval=math.log(S5_DT_MIN), maxval=math.log(S5_DT_MAX))
    s5_b_re = nrm((NA, G, P, H), (2 * H) ** -0.5)
    s5_b_im = nrm((NA, G, P, H), (2 * H) ** -0.5)
    s5_c_re = nrm((NA, G, H, P), (2 * P) ** -0.5)
    s5_c_im = nrm((NA, G, H, P), (2 * P) ** -0.5)
    s5_d = nrm((NA, S5_WIDTH), 1.0)
    s5_w_glu = nrm((NA, S5_WIDTH, 2 * D), S5_WIDTH ** -0.5)
    s5_b_glu = nrm((NA, 2 * D), 0.02)
    m2_w_in = nrm((NB, D, M2_IN_DIM), D ** -0.5)
    m2_conv_w = nrm((NB, M2_CONV, M2_CONV_DIM), M2_CONV ** -0.5)
    m2_conv_b = nrm((NB, M2_CONV_DIM), 0.02)
    dt0 = jnp.exp(jax.random.uniform(next(it), (NB, M2_HEADS), f32, minval=math.log(M2_DT_MIN), maxval=math.log(M2_DT_MAX)))
    m2_dt_bias = dt0 + jnp.log(-jnp.expm1(-dt0))
    m2_a_log = jnp.log(jax.random.uniform(next(it), (NB, M2_HEADS), f32, minval=1.0, maxval=16.0))
    m2_d = 1.0 + nrm((NB, M2_HEADS), 0.1)
    m2_norm_g = 1.0 + nrm((NB, M2_D_INNER), 0.05)
    m2_w_out = nrm((NB, M2_D_INNER, D), M2_D_INNER ** -0.5)
    final_norm_g = 1.0 + nrm((D,), 0.05)
    return {'x': x, 'c': c, 'ada_w': ada_w, 'ada_b': ada_b,
            'norm_mix_g': norm_mix_g, 'norm_mlp_g': norm_mlp_g,
            'mlp_w1': mlp_w1, 'mlp_w2': mlp_w2,
            's5_w_in': s5_w_in, 's5_lambda_re': s5_lambda_re, 's5_lambda_im': s5_lambda_im,
            's5_log_dt': s5_log_dt, 's5_b_re': s5_b_re, 's5_b_im': s5_b_im,
            's5_c_re': s5_c_re, 's5_c_im': s5_c_im, 's5_d': s5_d,
            's5_w_glu': s5_w_glu, 's5_b_glu': s5_b_glu,
            'm2_w_in': m2_w_in, 'm2_conv_w': m2_conv_w, 'm2_conv_b': m2_conv_b,
            'm2_dt_bias': m2_dt_bias, 'm2_a_log': m2_a_log, 'm2_d': m2_d,
            'm2_norm_g': m2_norm_g, 'm2_w_out': m2_w_out,
            'final_norm_g': final_norm_g}


def reference(x, c, ada_w, ada_b, norm_mix_g, norm_mlp_g, mlp_w1, mlp_w2,
              s5_w_in, s5_lambda_re, s5_lambda_im, s5_log_dt, s5_b_re, s5_b_im,
              s5_c_re, s5_c_im, s5_d, s5_w_glu, s5_b_glu,
              m2_w_in, m2_conv_w, m2_conv_b, m2_dt_bias, m2_a_log, m2_d,
              m2_norm_g, m2_w_out, final_norm_g):
    cond = jax.nn.silu(c)
    for i in range(DEPTH):
        mod = cond @ ada_w[i] + ada_b[i]
        sh1, sc1, g1, sh2, sc2, g2 = jnp.split(mod[:, None, :], N_MOD, axis=-1)
        h = _modulate(_rmsnorm(x, norm_mix_g[i]), sh1, sc1)
        j = i // N_MIXERS
        if i % N_MIXERS == 0:
            y = _s5_mixer(h, s5_w_in[j], s5_lambda_re[j], s5_lambda_im[j], s5_log_dt[j],
                          s5_b_re[j], s5_b_im[j], s5_c_re[j], s5_c_im[j], s5_d[j],
                          s5_w_glu[j], s5_b_glu[j])
        else:
            y = _ssd_mixer(h, m2_w_in[j], m2_conv_w[j], m2_conv_b[j], m2_dt_bias[j],
                           m2_a_log[j], m2_d[j], m2_norm_g[j], m2_w_out[j])
        x = x + g1 * y
        h = _modulate(_rmsnorm(x, norm_mlp_g[i]), sh2, sc2)
        x = x + g2 * _sq_relu_mlp(h, mlp_w1[i], mlp_w2[i])
    return _rmsnorm(x, final_norm_g)
```

```python
import math
from contextlib import ExitStack
import numpy as np
import concourse.bass as bass
import concourse.mybir as mybir
from concourse.bass_utils import run_bass_kernel_spmd

F32 = mybir.dt.float32
BF16 = mybir.dt.bfloat16
ALU = mybir.AluOpType
AF = mybir.ActivationFunctionType

ENGS = ("tensor", "vector", "scalar", "gpsimd", "sync")
EPOCH = 12000
NDMASEM = 24
EPS = 1e-5
D = 1024
DFF = 4096
TWO_PI = 2.0 * math.pi
SIM_MODE = False


class Buf:
    __slots__ = ("name", "w", "r")

    def __init__(self, name=""):
        self.name = name
        self.w = None
        self.r = []


class Prog:
    def __init__(self, nc, es):
        self.nc = nc
        self.es = es
        self.ops = {e: [] for e in ENGS}
        self.count = {e: 0 for e in ENGS}
        self.sems = {e: [] for e in ENGS}
        self.waited = {e: {} for e in ENGS}
        self.dsem = [es.enter_context(nc.semaphore(f"dma{i}")) for i in range(NDMASEM)]
        self.dcnt = [0] * NDMASEM
        self.dnext = 0

    def _eng_sem(self, e, epoch):
        while len(self.sems[e]) <= epoch:
            s = self.es.enter_context(self.nc.semaphore(f"s_{e}_{len(self.sems[e])}"))
            self.sems[e].append(s)
        return self.sems[e][epoch]

    def _waits(self, e, reads, writes, skip_same):
        toks = []
        for b in reads:
            if b.w is not None:
                toks.append(b.w)
        for b in writes:
            if b.w is not None:
                toks.append(b.w)
            toks.extend(b.r)
        wd = self.waited[e]
        best = {}
        for (s, v, src) in toks:
            if skip_same and src == e:
                continue
            k = id(s)
            if wd.get(k, 0) >= v:
                continue
            if k not in best or best[k][1] < v:
                best[k] = (s, v)
        for k, (s, v) in best.items():
            wd[k] = v
        return list(best.values())

    def _commit(self, tok, reads, writes):
        for b in reads:
            b.r.append(tok)
            if len(b.r) > 64:
                b.r = b.r[-48:]
        for b in writes:
            b.w = tok
            b.r = []

    def op(self, e, fn, reads=(), writes=()):
        waits = self._waits(e, reads, writes, e == "tensor")
        n = self.count[e]
        self.count[e] = n + 1
        sem = self._eng_sem(e, n // EPOCH)
        tok = (sem, n % EPOCH + 1, e)
        self.ops[e].append((fn, waits, sem, 1))
        self._commit(tok, reads, writes)
        return tok

    def dma(self, q, fn, reads=(), writes=()):
        if q == "gpsimd" and SIM_MODE:
            sem = self.es.enter_context(self.nc.semaphore(f"sw{len(self.dsem)}"))
            self.dsem.append(sem)
            self.dcnt.append(0)
            i = len(self.dsem) - 1
        else:
            i = self.dnext
            self.dnext = (i + 1) % NDMASEM
            sem = self.dsem[i]
        waits = self._waits(q, reads, writes, False)
        if self.dcnt[i] > 0:
            k = id(sem)
            if self.waited[q].get(k, 0) < self.dcnt[i]:
                self.waited[q][k] = self.dcnt[i]
                waits.append((sem, self.dcnt[i]))
        self.dcnt[i] += 16
        tok = (sem, self.dcnt[i], "dma")
        self.ops[q].append((fn, waits, sem, 16))
        self._commit(tok, reads, writes)
        return tok

    def barrier(self):
        toks = []
        for e in ENGS:
            n = self.count[e]
            if n > 0:
                toks.append((self.sems[e][(n - 1) // EPOCH], (n - 1) % EPOCH + 1, e))
        for i in range(len(self.dsem)):
            if self.dcnt[i] > 0:
                toks.append((self.dsem[i], self.dcnt[i], "dma"))
        for e in ENGS:
            wd = self.waited[e]
            ws = []
            for (s, v, src) in toks:
                if src == e:
                    continue
                kk = id(s)
                if wd.get(kk, 0) >= v:
                    continue
                wd[kk] = v
                ws.append((s, v))
            if ws:
                self.ops[e].append((None, ws, None, 0))

    def finish(self):
        self.ops["sync"].append((None, [(self.dsem[i], self.dcnt[i]) for i in range(len(self.dsem)) if self.dcnt[i] > 0], None, 0))

    def emit(self):
        nc = self.nc
        with nc.Block() as block:
            def run(eng, lst):
                for fn, waits, sem, inc in lst:
                    for s, v in waits:
                        eng.wait_ge(s, v)
                    if fn is not None:
                        fn(eng).then_inc(sem, inc)

            @block.tensor
            def _(eng):
                run(eng, self.ops["tensor"])

            @block.vector
            def _(eng):
                run(eng, self.ops["vector"])

            @block.scalar
            def _(eng):
                run(eng, self.ops["scalar"])

            @block.gpsimd
            def _(eng):
                run(eng, self.ops["gpsimd"])

            @block.sync
            def _(eng):
                run(eng, self.ops["sync"])
        self.ops = {e: [] for e in ENGS}


class T:
    def __init__(self, t, name, nb=1):
        self.t = t
        self.b = [Buf(f"{name}{i}") for i in range(nb)]

    def __getitem__(self, k):
        return self.t[k]


class K:
    def __init__(self, nc, es):
        self.nc = nc
        self.es = es
        self.P = Prog(nc, es)
        self.uid = 0
        self.din = {}

    def sb(self, name, shape, dt, nb=1):
        self.uid += 1
        t = self.es.enter_context(self.nc.sbuf_tensor(f"{name}_{self.uid}", list(shape), dt))
        return T(t, name, nb)

    def ps(self, name, shape, dt=F32, nb=1):
        self.uid += 1
        t = self.es.enter_context(self.nc.psum_tensor(f"{name}_{self.uid}", list(shape), dt))
        return T(t, name, nb)

    def dram_in(self, name, shape, dt=F32):
        if name in self.din:
            return self.din[name]
        ap = self.nc.dram_tensor(name, list(shape), dt, kind="ExternalInput").ap()
        self.din[name] = ap
        return ap

    def dram_out(self, name, shape, dt=F32):
        return self.nc.dram_tensor(name, list(shape), dt, kind="ExternalOutput").ap()

    def dram_tmp(self, name, shape, dt):
        self.uid += 1
        return self.nc.dram_tensor(f"{name}_{self.uid}", list(shape), dt, kind="Internal").ap()

    def phase(self):
        k = self

        class _Ph:
            def __enter__(self_):
                self_.old = k.es
                self_.st = ExitStack()
                self_.st.__enter__()
                k.es = self_.st
                return self_

            def __exit__(self_, *a):
                if a[0] is None:
                    k.P.barrier()
                    k.P.emit()
                k.es = self_.old
                return self_.st.__exit__(*a)
        return _Ph()

    def V(self, fn, r=(), w=()):
        return self.P.op("vector", fn, r, w)

    def A(self, fn, r=(), w=()):
        return self.P.op("scalar", fn, r, w)

    def G(self, fn, r=(), w=()):
        return self.P.op("gpsimd", fn, r, w)

    def PE(self, fn, r=(), w=()):
        return self.P.op("tensor", fn, r, w)

    def DMA(self, q, out, in_, r=(), w=()):
        return self.P.dma(q, lambda e: e.dma_start(out=out, in_=in_), r, w)


class Common:
    def __init__(self, k: K):
        self.k = k
        nc = k.nc
        self.pb = [k.ps(f"pb{i}", [128, 512]) for i in range(8)]
        self.pbi = 0
        self.ident = k.sb("ident", [128, 128], F32)
        self.identb = k.sb("identb", [128, 128], BF16)
        self.U = k.sb("U", [128, 128], F32)
        self.onesD = k.sb("onesD", [128, 128], BF16)
        self.ones512 = k.sb("ones512", [128, 128], BF16)
        self.ones32 = k.sb("ones32", [128, 128], F32)
        c_ident = k.dram_in("c_ident", [128, 128])
        c_U = k.dram_in("c_U", [128, 128])
        k.DMA("sync", self.ident[:], c_ident, w=self.ident.b)
        k.DMA("gpsimd", self.identb[:], c_ident, w=self.identb.b)
        k.DMA("sync", self.U[:], c_U, w=self.U.b)
        k.V(lambda e: e.memset(self.onesD[:], 1.0 / 1024.0), w=self.onesD.b)
        k.V(lambda e: e.memset(self.ones512[:], 1.0 / 512.0), w=self.ones512.b)
        k.V(lambda e: e.memset(self.ones32[:], 1.0), w=self.ones32.b)
        self.sqb = [k.sb("sqb", [128, 512], BF16) for _ in range(2)]
        self.rs = k.sb("rs", [128, 512], F32)
        self.rstd = k.sb("rstd", [128, 512], F32)
        self.ntmp = [k.sb("ntmp", [128, 512], F32) for _ in range(2)]
        self.rot = 0

    def bank(self):
        b = self.pb[self.pbi]
        self.pbi = (self.pbi + 1) % 8
        return b

    def adaln(self, li):
        k = self.k
        c_col = k.dram_in("c_col", [128, 8])
        ada_w = k.dram_in(f"ada_w{li}", [D, 6 * D])
        ada_b = k.dram_in(f"ada_b{li}", [128, 48])
        gmix = k.dram_in(f"gmix{li}", [128, 8])
        gmlp = k.dram_in(f"gmlp{li}", [128, 8])
        mod = k.sb("mod", [128, 48], F32)
        out = k.sb("modout", [128, 16], F32)
        with k.phase():
            self._adaln_body(li, c_col, ada_w, ada_b, gmix, gmlp, mod, out)
        return dict(mod=mod, gs=out)

    def _adaln_body(self, li, c_col, ada_w, ada_b, gmix, gmlp, mod, out):
        k = self.k
        cc = k.sb("ccol", [128, 8], F32)
        cond = k.sb("cond", [128, 8], F32)
        adab = k.sb("adab", [128, 48], F32)
        gm = k.sb("gm", [128, 16], F32)
        k.DMA("sync", cc[:], c_col, w=cc.b)
        k.DMA("sync", adab[:], ada_b, w=adab.b)
        k.DMA("sync", gm[:, 0:8], gmix, w=gm.b)
        k.DMA("sync", gm[:, 8:16], gmlp, w=gm.b)
        k.A(lambda e: e.activation(out=cond[:], in_=cc[:], func=AF.Silu), r=cc.b, w=cond.b)
        wv = ada_w.rearrange("(k p) f -> p k f", p=128)
        stg = [k.sb("adastg", [128, 8, 512], F32) for _ in range(2)]
        pm = self.bank()
        for pc in range(12):
            s = stg[pc % 2]
            k.DMA("sync", s[:], wv[:, :, pc * 512:(pc + 1) * 512], w=s.b)
            for j in range(4):
                col = pc * 4 + j
                for kk in range(8):
                    k.PE(lambda e, s=s, j=j, kk=kk, col=col: e.matmul(
                        pm[:, col:col + 1], lhsT=s[:, kk, j * 128:(j + 1) * 128], rhs=cond[:, kk:kk + 1],
                        start=(kk == 0), stop=(kk == 7)), r=s.b + cond.b, w=pm.b)
        k.V(lambda e: e.tensor_tensor(out=mod[:], in0=pm[:, 0:48], in1=adab[:], op=ALU.add), r=pm.b + adab.b, w=mod.b)
        k.V(lambda e: e.scalar_tensor_tensor(out=out[:, 0:8], in0=mod[:, 8:16], scalar=1.0, in1=gm[:, 0:8],
                                             op0=ALU.add, op1=ALU.mult), r=mod.b + gm.b, w=out.b)
        k.V(lambda e: e.scalar_tensor_tensor(out=out[:, 8:16], in0=mod[:, 32:40], scalar=1.0, in1=gm[:, 8:16],
                                             op0=ALU.add, op1=ALU.mult), r=mod.b + gm.b, w=out.b)

    def stats_rstd(self, chunks, chunk_bufs, TT, ones, eps=EPS):
        k = self.k
        pst = self.bank()
        n = len(chunks)
        for i, (ap, bb) in enumerate(zip(chunks, chunk_bufs)):
            sq = self.sqb[self.rot % 2]
            self.rot += 1
            k.A(lambda e, sq=sq, ap=ap: e.activation(out=sq[:, :TT], in_=ap, func=AF.Square), r=bb, w=sq.b)
            k.PE(lambda e, sq=sq, i=i: e.matmul(pst[:, :TT], lhsT=ones[:], rhs=sq[:, :TT], start=(i == 0), stop=(i == n - 1)),
                 r=sq.b + ones.b, w=pst.b)
        k.A(lambda e: e.activation(out=self.rs[:, :TT], in_=pst[:, :TT], func=AF.Sqrt, bias=eps, scale=1.0), r=pst.b, w=self.rs.b)
        k.V(lambda e: e.reciprocal(out=self.rstd[:, :TT], in_=self.rs[:, :TT]), r=self.rs.b, w=self.rstd.b)
        return self.rstd

    def norm_mod(self, xt, TT, gs, gs0, sh, sh0, h):
        k = self.k
        rstd = self.stats_rstd([xt[:, kk, :TT] for kk in range(8)], [xt.b] * 8, TT, self.onesD)
        for kk in range(8):
            tmp = self.ntmp[kk % 2]
            k.V(lambda e, kk=kk, tmp=tmp: e.scalar_tensor_tensor(
                out=tmp[:, :TT], in0=xt[:, kk, :TT], scalar=gs[:, gs0 + kk:gs0 + kk + 1], in1=rstd[:, :TT],
                op0=ALU.mult, op1=ALU.mult), r=xt.b + gs.b + rstd.b, w=tmp.b)
            k.A(lambda e, kk=kk, tmp=tmp: e.activation(out=h[:, kk, :TT], in_=tmp[:, :TT], func=AF.Identity,
                                                       bias=sh[:, sh0 + kk:sh0 + kk + 1], scale=1.0),
                r=tmp.b + sh.b, w=h.b)


def load_w_bf16(k: K, dst: T, src_ap, nk, ncols, piece=None):
    v = src_ap.rearrange("(k p) f -> p k f", p=128)
    kk_per = piece or max(1, 8192 // ncols)
    for k0 in range(0, nk, kk_per):
        k1 = min(nk, k0 + kk_per)
        k.DMA("gpsimd", dst[:, k0:k1, 0:ncols], v[:, k0:k1, :], w=dst.b)


def phase_ffn(k: K, cm: Common, li, NT, md, x_src, x_dst, final_g=None, TT=512):
    w1 = k.dram_in(f"mlp_w1_{li}", [D, DFF])
    w2 = k.dram_in(f"mlp_w2_{li}", [DFF, D])
    with k.phase():
        w1b = k.sb("w1b", [128, 8, DFF], BF16)
        w2b = k.sb("w2b", [128, 32, D], BF16)
        load_w_bf16(k, w1b, w1, 8, DFF, piece=2)
        load_w_bf16(k, w2b, w2, 32, D, piece=8)
        xts = [k.sb("xt", [128, 8, TT], F32) for _ in range(1)]
        h = k.sb("h", [128, 8, TT], BF16)
        a2 = k.sb("a2", [128, 32, TT], BF16)
        r32 = [k.sb("r32", [128, TT], F32) for _ in range(2)]
        mod, gs = md["mod"], md["gs"]
        xs_v = x_src.rearrange("(k p) t -> p k t", p=128)
        xd_v = x_dst.rearrange("(k p) t -> p k t", p=128)
        gfc = None
        if final_g is not None:
            gfd = k.dram_in("gfinal", [128, 8])
            gfc = k.sb("gfc", [128, 8], F32)
            k.DMA("sync", gfc[:], gfd, w=gfc.b)
        nt = NT // TT
        for ti in range(nt):
            xt = xts[0]
            t0 = ti * TT
            k.DMA("sync", xt[:], xs_v[:, :, t0:t0 + TT], w=xt.b)
            cm.norm_mod(xt, TT, gs, 8, mod, 24, h)
            for j in range(32):
                pb = cm.bank()
                for kk in range(8):
                    k.PE(lambda e, pb=pb, j=j, kk=kk: e.matmul(pb[:, :TT], lhsT=w1b[:, kk, j * 128:(j + 1) * 128],
                                                               rhs=h[:, kk, :], start=(kk == 0), stop=(kk == 7)),
                         r=w1b.b + h.b, w=pb.b)
                rr = r32[j % 2]
                k.A(lambda e, pb=pb, rr=rr: e.activation(out=rr[:], in_=pb[:, :TT], func=AF.Relu), r=pb.b, w=rr.b)
                k.V(lambda e, rr=rr, j=j: e.tensor_tensor(out=a2[:, j, :], in0=rr[:], in1=rr[:], op=ALU.mult), r=rr.b, w=a2.b)
            for m in range(8):
                pb = cm.bank()
                for j in range(32):
                    k.PE(lambda e, pb=pb, j=j, m=m: e.matmul(pb[:, :TT], lhsT=w2b[:, j, m * 128:(m + 1) * 128],
                                                             rhs=a2[:, j, :], start=(j == 0), stop=(j == 31)),
                         r=w2b.b + a2.b, w=pb.b)
                k.V(lambda e, pb=pb, m=m, xt=xt: e.scalar_tensor_tensor(
                    out=xt[:, m, :], in0=pb[:, :TT], scalar=mod[:, 40 + m:41 + m], in1=xt[:, m, :],
                    op0=ALU.mult, op1=ALU.add), r=pb.b + mod.b + xt.b, w=xt.b)
            if gfc is not None:
                rstd = cm.stats_rstd([xt[:, kk, :] for kk in range(8)], [xt.b] * 8, TT, cm.onesD)
                for kk in range(8):
                    k.V(lambda e, kk=kk, xt=xt: e.scalar_tensor_tensor(
                        out=xt[:, kk, :], in0=xt[:, kk, :], scalar=gfc[:, kk:kk + 1], in1=rstd[:, :TT],
                        op0=ALU.mult, op1=ALU.mult), r=xt.b + gfc.b + rstd.b, w=xt.b)
            k.DMA("sync", xd_v[:, :, t0:t0 + TT], xt[:], r=xt.b)


def col(v):
    v = np.asarray(v, dtype=np.float32).reshape(-1, 128)
    return np.ascontiguousarray(v.T)


def const_inputs():
    ident = np.eye(128, dtype=np.float32)
    U = np.triu(np.ones((128, 128), dtype=np.float32))
    return {"c_ident": ident, "c_U": U}


NLV = 10


def s5_host_inputs(j, inp, pre=""):
    def st(a):
        return np.ascontiguousarray(np.asarray(a, np.float32).reshape(32, 2, 64).transpose(1, 2, 0).reshape(128, 32))
    ld = np.broadcast_to(np.asarray(inp["s5_log_dt"][j], np.float32).reshape(32, 2, 1), (32, 2, 64))
    d = {
        f"{pre}s5_w_in": np.asarray(inp["s5_w_in"][j], np.float32),
        f"{pre}s5_w_glu": np.asarray(inp["s5_w_glu"][j], np.float32),
        f"{pre}s5_lre": st(inp["s5_lambda_re"][j]),
        f"{pre}s5_lim": st(inp["s5_lambda_im"][j]),
        f"{pre}s5_ldt": np.ascontiguousarray(ld.transpose(1, 2, 0).reshape(128, 32)),
        f"{pre}s5_bre": np.ascontiguousarray(np.asarray(inp["s5_b_re"][j], np.float32).reshape(32, 2, 64, 16).transpose(1, 2, 0, 3).reshape(128, 32, 16)),
        f"{pre}s5_bim": np.ascontiguousarray(np.asarray(inp["s5_b_im"][j], np.float32).reshape(32, 2, 64, 16).transpose(1, 2, 0, 3).reshape(128, 32, 16)),
        f"{pre}s5_cre": np.ascontiguousarray(np.asarray(inp["s5_c_re"][j], np.float32).reshape(32, 2, 16, 64).transpose(1, 3, 0, 2).reshape(128, 32, 16)),
        f"{pre}s5_cim": np.ascontiguousarray(np.asarray(inp["s5_c_im"][j], np.float32).reshape(32, 2, 16, 64).transpose(1, 3, 0, 2).reshape(128, 32, 16)),
        f"{pre}s5_d": col(inp["s5_d"][j]),
        f"{pre}s5_bglu": col(inp["s5_b_glu"][j]),
    }
    return d


def phase_s5(k: K, cm: Common, pre, NT, md, x_src, x_dst, st_in, st_out, TT=512):
    din = lambda n, shp: k.dram_in(f"{pre}{n}", shp)
    w_in = din("s5_w_in", [D, D]); w_glu = din("s5_w_glu", [D, 2 * D])
    lre_d = din("s5_lre", [128, 32]); lim_d = din("s5_lim", [128, 32]); ldt_d = din("s5_ldt", [128, 32])
    bre_d = din("s5_bre", [128, 32, 16]); bim_d = din("s5_bim", [128, 32, 16])
    cre_d = din("s5_cre", [128, 32, 16]); cim_d = din("s5_cim", [128, 32, 16])
    d_d = din("s5_d", [128, 8]); bglu_d = din("s5_bglu", [128, 16])
    mod, gs = md["mod"], md["gs"]
    outer_es = k.es
    with ExitStack() as layer_es:
        k.es = layer_es
        WB = k.sb("WB", [128, 8, 4, 2, 128], BF16)
        WC = k.sb("WC", [128, 8, 4, 2, 128], BF16)
        lam = k.sb("lam", [128, 32, NLV, 3], F32)
        carry = k.sb("carry", [128, 32, 2], F32, nb=32)
        dcol = k.sb("dcol", [128, 8], F32); bglu = k.sb("bglu", [128, 16], F32)
        k.DMA("sync", dcol[:], d_d, w=dcol.b); k.DMA("sync", bglu[:], bglu_d, w=bglu.b)
        k.DMA("sync", carry[:], st_in, w=carry.b)
        with k.phase():
            lre = k.sb("lre", [128, 32], F32); lim = k.sb("lim", [128, 32], F32); ldt = k.sb("ldt", [128, 32], F32)
            bre = k.sb("bre", [128, 32, 16], F32); bim = k.sb("bim", [128, 32, 16], F32)
            cre = k.sb("cre", [128, 32, 16], F32); cim = k.sb("cim", [128, 32, 16], F32)
            for t_, d_ in ((lre, lre_d), (lim, lim_d), (ldt, ldt_d), (bre, bre_d), (bim, bim_d), (cre, cre_d), (cim, cim_d)):
                k.DMA("sync", t_[:], d_, w=t_.b)
            sc = k.sb("s5sc", [128, 12, 32], F32)
            S = lambda i: sc[:, i, :]
            V = k.V
            A = k.A
            A(lambda e: e.activation(out=S(0), in_=ldt[:], func=AF.Exp), r=ldt.b, w=sc.b)
            V(lambda e: e.tensor_tensor(out=S(1), in0=lre[:], in1=S(0), op=ALU.mult), r=lre.b + sc.b, w=sc.b)
            V(lambda e: e.tensor_tensor(out=S(2), in0=lim[:], in1=S(0), op=ALU.mult), r=lim.b + sc.b, w=sc.b)
            A(lambda e: e.activation(out=S(3), in_=S(1), func=AF.Exp), r=sc.b, w=sc.b)
            A(lambda e: e.activation(out=S(4), in_=S(2), func=AF.Sin, scale=1.0 / 32.0), r=sc.b, w=sc.b)
            A(lambda e: e.activation(out=S(5), in_=S(2), func=AF.Sin, scale=1.0 / 32.0, bias=math.pi / 2), r=sc.b, w=sc.b)
            for _ in range(5):
                V(lambda e: e.tensor_tensor(out=S(6), in0=S(4), in1=S(5), op=ALU.mult), r=sc.b, w=sc.b)
                V(lambda e: e.tensor_tensor(out=S(7), in0=S(4), in1=S(4), op=ALU.mult), r=sc.b, w=sc.b)
                V(lambda e: e.tensor_tensor(out=S(8), in0=S(5), in1=S(5), op=ALU.mult), r=sc.b, w=sc.b)
                V(lambda e: e.tensor_scalar(out=S(4), in0=S(6), scalar1=2.0, scalar2=0.0, op0=ALU.mult, op1=ALU.add), r=sc.b, w=sc.b)
                V(lambda e: e.tensor_tensor(out=S(5), in0=S(8), in1=S(7), op=ALU.subtract), r=sc.b, w=sc.b)
            V(lambda e: e.tensor_tensor(out=lam[:, :, 0, 0], in0=S(3), in1=S(5), op=ALU.mult), r=sc.b, w=lam.b)
            V(lambda e: e.tensor_tensor(out=lam[:, :, 0, 1], in0=S(3), in1=S(4), op=ALU.mult), r=sc.b, w=lam.b)
            for m in range(1, NLV):
                V(lambda e, m=m: e.tensor_tensor(out=S(6), in0=lam[:, :, m - 1, 0], in1=lam[:, :, m - 1, 0], op=ALU.mult), r=lam.b, w=sc.b)
                V(lambda e, m=m: e.tensor_tensor(out=S(7), in0=lam[:, :, m - 1, 1], in1=lam[:, :, m - 1, 1], op=ALU.mult), r=lam.b, w=sc.b)
                V(lambda e, m=m: e.tensor_tensor(out=S(8), in0=lam[:, :, m - 1, 0], in1=lam[:, :, m - 1, 1], op=ALU.mult), r=lam.b, w=sc.b)
                V(lambda e, m=m: e.tensor_tensor(out=lam[:, :, m, 0], in0=S(6), in1=S(7), op=ALU.subtract), r=sc.b, w=lam.b)
                V(lambda e, m=m: e.tensor_scalar(out=lam[:, :, m, 1], in0=S(8), scalar1=2.0, scalar2=0.0, op0=ALU.mult, op1=ALU.add), r=sc.b, w=lam.b)
            V(lambda e: e.tensor_scalar(out=lam[:, :, :, 2], in0=lam[:, :, :, 1], scalar1=-1.0, scalar2=0.0, op0=ALU.mult, op1=ALU.add), r=lam.b, w=lam.b)
            V(lambda e: e.tensor_tensor(out=S(6), in0=lre[:], in1=lre[:], op=ALU.mult), r=lre.b, w=sc.b)
            V(lambda e: e.tensor_tensor(out=S(7), in0=lim[:], in1=lim[:], op=ALU.mult), r=lim.b, w=sc.b)
            V(lambda e: e.tensor_tensor(out=S(6), in0=S(6), in1=S(7), op=ALU.add), r=sc.b, w=sc.b)
            V(lambda e: e.reciprocal(out=S(6), in_=S(6)), r=sc.b, w=sc.b)
            V(lambda e: e.tensor_scalar(out=S(7), in0=lam[:, :, 0, 0], scalar1=-1.0, scalar2=0.0, op0=ALU.add, op1=ALU.add), r=lam.b, w=sc.b)
            V(lambda e: e.tensor_tensor(out=S(8), in0=S(7), in1=lre[:], op=ALU.mult), r=sc.b + lre.b, w=sc.b)
            V(lambda e: e.tensor_tensor(out=S(9), in0=lam[:, :, 0, 1], in1=lim[:], op=ALU.mult), r=lam.b + lim.b, w=sc.b)
            V(lambda e: e.tensor_tensor(out=S(8), in0=S(8), in1=S(9), op=ALU.add), r=sc.b, w=sc.b)
            V(lambda e: e.tensor_tensor(out=S(10), in0=S(8), in1=S(6), op=ALU.mult), r=sc.b, w=sc.b)
            V(lambda e: e.tensor_tensor(out=S(8), in0=lam[:, :, 0, 1], in1=lre[:], op=ALU.mult), r=lam.b + lre.b, w=sc.b)
            V(lambda e: e.tensor_tensor(out=S(9), in0=S(7), in1=lim[:], op=ALU.mult), r=sc.b + lim.b, w=sc.b)
            V(lambda e: e.tensor_tensor(out=S(8), in0=S(8), in1=S(9), op=ALU.subtract), r=sc.b, w=sc.b)
            V(lambda e: e.tensor_tensor(out=S(11), in0=S(8), in1=S(6), op=ALU.mult), r=sc.b, w=sc.b)
            bbr = k.sb("bbr", [128, 32, 16], F32); bbi = k.sb("bbi", [128, 32, 16], F32); tq = k.sb("tq", [128, 32, 16], F32)
            nrb = sc[:, 10, :].unsqueeze(2).to_broadcast([128, 32, 16])
            nib = sc[:, 11, :].unsqueeze(2).to_broadcast([128, 32, 16])
            V(lambda e: e.tensor_tensor(out=bbr[:], in0=bre[:], in1=nrb, op=ALU.mult), r=bre.b + sc.b, w=bbr.b)
            V(lambda e: e.tensor_tensor(out=tq[:], in0=bim[:], in1=nib, op=ALU.mult), r=bim.b + sc.b, w=tq.b)
            V(lambda e: e.tensor_tensor(out=bbr[:], in0=bbr[:], in1=tq[:], op=ALU.subtract), r=bbr.b + tq.b, w=bbr.b)
            V(lambda e: e.tensor_tensor(out=bbi[:], in0=bim[:], in1=nrb, op=ALU.mult), r=bim.b + sc.b, w=bbi.b)
            V(lambda e: e.tensor_tensor(out=tq[:], in0=bre[:], in1=nib, op=ALU.mult), r=bre.b + sc.b, w=tq.b)
            V(lambda e: e.tensor_tensor(out=bbi[:], in0=bbi[:], in1=tq[:], op=ALU.add), r=bbi.b + tq.b, w=bbi.b)
            if getattr(k, "debug", False):
                dbg_sc = k.dram_out("dbg_sc", [128, 12, 32])
                dbg_bbr = k.dram_out("dbg_bbr", [128, 32, 16])
                k.DMA("sync", dbg_sc, sc[:], r=sc.b)
                k.DMA("sync", dbg_bbr, bbr[:], r=bbr.b)
            V(lambda e: e.tensor_scalar(out=cim[:], in0=cim[:], scalar1=-1.0, scalar2=0.0, op0=ALU.mult, op1=ALU.add), r=cim.b, w=cim.b)
            k.G(lambda e: e.memset(WC[:], 0.0), w=WC.b)
            for q in range(32):
                f, qq = q // 4, q % 4
                for ri, src in ((0, cre), (1, cim)):
                    for g2 in range(2):
                        c0 = 32 * qq + 16 * g2
                        k.A(lambda e, f=f, qq=qq, ri=ri, src=src, g2=g2, c0=c0, q=q: e.activation(
                            out=WC[64 * g2:64 * g2 + 64, f, qq, ri, c0:c0 + 16], in_=src[64 * g2:64 * g2 + 64, q, :], func=AF.Identity),
                            r=src.b, w=WC.b)
            Xb = [[k.sb("Xb", [128, 128], F32) for _ in range(2)] for _ in range(4)]
            for qq in range(4):
                for ri in range(2):
                    k.G(lambda e, t_=Xb[qq][ri]: e.memset(t_[:], 0.0), w=Xb[qq][ri].b)
            for q in range(32):
                f, qq = q // 4, q % 4
                for ri, src in ((0, bbr), (1, bbi)):
                    X = Xb[qq][ri]
                    for g2 in range(2):
                        c0 = 32 * qq + 16 * g2
                        k.G(lambda e, X=X, g2=g2, c0=c0, src=src, q=q: e.tensor_copy(
                            out=X[64 * g2:64 * g2 + 64, c0:c0 + 16], in_=src[64 * g2:64 * g2 + 64, q, :]), r=src.b, w=X.b)
                    pb = cm.bank()
                    k.PE(lambda e, pb=pb, X=X: e.transpose(pb[:, 0:128], X[:], cm.ident[:]), r=X.b + cm.ident.b, w=pb.b)
                    k.A(lambda e, pb=pb, f=f, qq=qq, ri=ri: e.activation(out=WB[:, f, qq, ri, :], in_=pb[:, 0:128], func=AF.Identity),
                        r=pb.b, w=WB.b)
        if getattr(k, "debug", False):
            dbg_lam = k.dram_out("dbg_lam", [128, 32, NLV, 3])
            dbg_wb = k.dram_out("dbg_wb", [128, 8, 4, 2, 128], BF16)
            dbg_wc = k.dram_out("dbg_wc", [128, 8, 4, 2, 128], BF16)
            k.DMA("sync", dbg_lam, lam[:], r=lam.b)
            k.DMA("sync", dbg_wb, WB[:], r=WB.b)
            k.DMA("sync", dbg_wc, WC[:], r=WC.b)
        ph2 = k.phase()
        ph2.__enter__()
        winb = k.sb("winb", [128, 8, D], BF16)
        wglub = k.sb("wglub", [128, 8, 2 * D], BF16)
        load_w_bf16(k, winb, w_in, 8, D)
        load_w_bf16(k, wglub, w_glu, 8, 2 * D)
        xt = k.sb("xt", [128, 8, TT], F32)
        h = k.sb("h", [128, 8, TT], BF16)
        u32 = k.sb("u32", [128, 8, TT], F32)
        ubf = k.sb("ubf", [128, 8, TT], BF16)
        gbf = k.sb("gbf", [128, 8, TT], BF16)
        Hbf = [k.sb("Hbf", [128, 4, 2, TT], BF16, nb=2) for _ in range(2)]
        scan = [[[k.sb("scan", [128, TT], F32) for _ in range(2)] for _ in range(2)] for _ in range(2)]
        yv = [k.sb("yv", [128, TT], F32) for _ in range(2)]
        sig = [k.sb("sig", [128, TT], F32) for _ in range(2)]
        gt = [k.sb("gt", [128, TT], F32) for _ in range(2)]
        xs_v = x_src.rearrange("(k p) t -> p k t", p=128)
        xd_v = x_dst.rearrange("(k p) t -> p k t", p=128)
        nlev = int(math.log2(TT))
        for ti in range(NT // TT):
            t0 = ti * TT
            k.DMA("sync", xt[:], xs_v[:, :, t0:t0 + TT], w=xt.b)
            cm.norm_mod(xt, TT, gs, 0, mod, 0, h)
            for f in range(8):
                pb = cm.bank()
                for kk in range(8):
                    k.PE(lambda e, pb=pb, f=f, kk=kk: e.matmul(pb[:, :TT], lhsT=winb[:, kk, f * 128:(f + 1) * 128], rhs=h[:, kk, :],
                                                               start=(kk == 0), stop=(kk == 7)), r=winb.b + h.b, w=pb.b)
                k.A(lambda e, pb=pb, f=f: e.activation(out=u32[:, f, :], in_=pb[:, :TT], func=AF.Identity), r=pb.b, w=u32.b)
                k.G(lambda e, f=f: e.tensor_copy(out=ubf[:, f, :], in_=u32[:, f, :]), r=u32.b, w=ubf.b)
            for f in range(8):
                Hb = Hbf[f % 2]
                for qq in range(4):
                    q = 4 * f + qq
                    sset = scan[q % 2]
                    Ar, Ai = sset[0]
                    Br, Bi = sset[1]
                    pr = cm.bank(); pi = cm.bank()
                    k.PE(lambda e, pr=pr, f=f, qq=qq: e.matmul(pr[:, :TT], lhsT=WB[:, f, qq, 0, :], rhs=ubf[:, f, :], start=True, stop=True),
                         r=WB.b + ubf.b, w=pr.b)
                    k.PE(lambda e, pi=pi, f=f, qq=qq: e.matmul(pi[:, :TT], lhsT=WB[:, f, qq, 1, :], rhs=ubf[:, f, :], start=True, stop=True),
                         r=WB.b + ubf.b, w=pi.b)
                    k.A(lambda e, pr=pr, Ar=Ar: e.activation(out=Ar[:], in_=pr[:, :TT], func=AF.Identity), r=pr.b, w=Ar.b)
                    k.A(lambda e, pi=pi, Ai=Ai: e.activation(out=Ai[:], in_=pi[:, :TT], func=AF.Identity), r=pi.b, w=Ai.b)
                    l0 = lambda c, q=q: lam[:, q, 0, c:c + 1]
                    cr = carry[:, q, 0:1]; ci = carry[:, q, 1:2]
                    for (dst, a, sca) in ((Ar, cr, 0), (Ar, ci, 2), (Ai, cr, 1), (Ai, ci, 0)):
                        k.V(lambda e, dst=dst, a=a, sca=sca, l0=l0: e.scalar_tensor_tensor(
                            out=dst[:, 0:1], in0=a, scalar=l0(sca), in1=dst[:, 0:1], op0=ALU.mult, op1=ALU.add),
                            r=carry.b[q:q + 1] + lam.b + dst.b, w=dst.b)
                    src = (Ar, Ai); dst = (Br, Bi)
                    for m in range(nlev):
                        dd = 1 << m
                        L = lambda c, q=q, m=m: lam[:, q, m, c:c + 1]
                        sr, si = src
                        dr, di = dst
                        k.V(lambda e, sr=sr, dr=dr, dd=dd, L=L: e.scalar_tensor_tensor(
                            out=dr[:, dd:], in0=sr[:, :TT - dd], scalar=L(0), in1=sr[:, dd:], op0=ALU.mult, op1=ALU.add),
                            r=sr.b + lam.b, w=dr.b)
                        k.V(lambda e, si=si, dr=dr, dd=dd, L=L: e.scalar_tensor_tensor(
                            out=dr[:, dd:], in0=si[:, :TT - dd], scalar=L(2), in1=dr[:, dd:], op0=ALU.mult, op1=ALU.add),
                            r=si.b + lam.b + dr.b, w=dr.b)
                        k.V(lambda e, sr=sr, si=si, di=di, dd=dd, L=L: e.scalar_tensor_tensor(
                            out=di[:, dd:], in0=sr[:, :TT - dd], scalar=L(1), in1=si[:, dd:], op0=ALU.mult, op1=ALU.add),
                            r=sr.b + si.b + lam.b, w=di.b)
                        k.V(lambda e, si=si, di=di, dd=dd, L=L: e.scalar_tensor_tensor(
                            out=di[:, dd:], in0=si[:, :TT - dd], scalar=L(0), in1=di[:, dd:], op0=ALU.mult, op1=ALU.add),
                            r=si.b + lam.b + di.b, w=di.b)
                        k.G(lambda e, sr=sr, dr=dr, dd=dd: e.tensor_copy(out=dr[:, :dd], in_=sr[:, :dd]), r=sr.b, w=dr.b)
                        k.G(lambda e, si=si, di=di, dd=dd: e.tensor_copy(out=di[:, :dd], in_=si[:, :dd]), r=si.b, w=di.b)
                        src, dst = dst, src
                    Hr, Hi = src
                    k.A(lambda e, Hr=Hr, q=q: e.activation(out=carry[:, q, 0:1], in_=Hr[:, TT - 1:TT], func=AF.Identity), r=Hr.b, w=carry.b[q:q + 1])
                    k.A(lambda e, Hi=Hi, q=q: e.activation(out=carry[:, q, 1:2], in_=Hi[:, TT - 1:TT], func=AF.Identity), r=Hi.b, w=carry.b[q:q + 1])
                    k.A(lambda e, Hr=Hr, Hb=Hb, qq=qq: e.activation(out=Hb[:, qq, 0, :], in_=Hr[:], func=AF.Identity), r=Hr.b, w=Hb.b[0:1])
                    k.G(lambda e, Hi=Hi, Hb=Hb, qq=qq: e.tensor_copy(out=Hb[:, qq, 1, :], in_=Hi[:]), r=Hi.b, w=Hb.b[1:2])
                py = cm.bank()
                n = 0
                for qq in range(4):
                    for ri in range(2):
                        k.PE(lambda e, py=py, f=f, qq=qq, ri=ri, Hb=Hb, n=n: e.matmul(
                            py[:, :TT], lhsT=WC[:, f, qq, ri, :], rhs=Hb[:, qq, ri, :], start=(n == 0), stop=(n == 7)),
                            r=WC.b + Hb.b[ri:ri + 1], w=py.b)
                        n += 1
                y_ = yv[f % 2]
                k.V(lambda e, py=py, f=f, y_=y_: e.scalar_tensor_tensor(out=y_[:], in0=u32[:, f, :], scalar=dcol[:, f:f + 1], in1=py[:, :TT],
                                                                         op0=ALU.mult, op1=ALU.add), r=u32.b + dcol.b + py.b, w=y_.b)
                g1_ = gt[f % 2]; s1_ = sig[f % 2]
                k.G(lambda e, y_=y_, g1_=g1_: e.tensor_tensor(out=g1_[:], in0=y_[:], in1=y_[:], op=ALU.mult), r=y_.b, w=g1_.b)
                k.G(lambda e, g1_=g1_: e.tensor_scalar(out=g1_[:], in0=g1_[:], scalar1=0.044715, scalar2=1.0, op0=ALU.mult, op1=ALU.add), r=g1_.b, w=g1_.b)
                k.G(lambda e, y_=y_, g1_=g1_: e.tensor_tensor(out=g1_[:], in0=g1_[:], in1=y_[:], op=ALU.mult), r=y_.b + g1_.b, w=g1_.b)
                k.A(lambda e, g1_=g1_, s1_=s1_: e.activation(out=s1_[:], in_=g1_[:], func=AF.Sigmoid, scale=2.0 * math.sqrt(2.0 / math.pi)), r=g1_.b, w=s1_.b)
                k.G(lambda e, y_=y_, s1_=s1_, f=f: e.tensor_tensor(out=gbf[:, f, :], in0=y_[:], in1=s1_[:], op=ALU.mult), r=y_.b + s1_.b, w=gbf.b)
            for m in range(8):
                pv = cm.bank(); pg = cm.bank()
                for kk in range(8):
                    k.PE(lambda e, pv=pv, m=m, kk=kk: e.matmul(pv[:, :TT], lhsT=wglub[:, kk, m * 128:(m + 1) * 128], rhs=gbf[:, kk, :],
                                                               start=(kk == 0), stop=(kk == 7)), r=wglub.b + gbf.b, w=pv.b)
                for kk in range(8):
                    k.PE(lambda e, pg=pg, m=m, kk=kk: e.matmul(pg[:, :TT], lhsT=wglub[:, kk, D + m * 128:D + (m + 1) * 128], rhs=gbf[:, kk, :],
                                                               start=(kk == 0), stop=(kk == 7)), r=wglub.b + gbf.b, w=pg.b)
                sg = sig[m % 2]; g_ = gt[m % 2]
                k.A(lambda e, pg=pg, sg=sg, m=m: e.activation(out=sg[:], in_=pg[:, :TT], func=AF.Sigmoid, bias=bglu[:, 8 + m:9 + m], scale=1.0),
                    r=pg.b + bglu.b, w=sg.b)
                k.V(lambda e, pv=pv, sg=sg, g_=g_, m=m: e.scalar_tensor_tensor(out=g_[:], in0=pv[:, :TT], scalar=bglu[:, m:m + 1], in1=sg[:],
                                                                               op0=ALU.add, op1=ALU.mult), r=pv.b + bglu.b + sg.b, w=g_.b)
                k.V(lambda e, g_=g_, m=m: e.scalar_tensor_tensor(out=xt[:, m, :], in0=g_[:], scalar=mod[:, 16 + m:17 + m], in1=xt[:, m, :],
                                                                 op0=ALU.mult, op1=ALU.add), r=g_.b + mod.b + xt.b, w=xt.b)
            k.DMA("sync", xd_v[:, :, t0:t0 + TT], xt[:], r=xt.b)
        k.DMA("sync", st_out, carry[:], r=carry.b)
        ph2.__exit__(None, None, None)
        k.es = outer_es


M2_IN = 5152


def ssd_host_inputs(j, inp, pre=""):
    cw = np.asarray(inp["m2_conv_w"][j], np.float32)
    rep = lambda v: np.ascontiguousarray(np.broadcast_to(np.asarray(v, np.float32).reshape(1, -1), (128, len(v))))
    return {
        f"{pre}m2_w_in": np.asarray(inp["m2_w_in"][j], np.float32),
        f"{pre}m2_w_out": np.asarray(inp["m2_w_out"][j], np.float32),
        f"{pre}m2_cw": np.ascontiguousarray(cw.T.reshape(24, 128, 4).transpose(1, 0, 2)),
        f"{pre}m2_cb": col(inp["m2_conv_b"][j]),
        f"{pre}m2_dtb": np.asarray(inp["m2_dt_bias"][j], np.float32).reshape(32, 1),
        f"{pre}m2_alog": rep(inp["m2_a_log"][j]),
        f"{pre}m2_drow": rep(inp["m2_d"][j]),
        f"{pre}m2_ng": col(inp["m2_norm_g"][j]),
    }


def phase_ssd_a(k: K, cm: Common, pre, NT, md, x_src, sz_d, xbc_d, dt_fm, halo_in, halo_out, TT=512):
    din = lambda n, shp: k.dram_in(f"{pre}{n}", shp)
    w_in = din("m2_w_in", [D, M2_IN])
    cw_d = din("m2_cw", [128, 24, 4]); cb_d = din("m2_cb", [128, 24]); dtb_d = din("m2_dtb", [32, 1])
    mod, gs = md["mod"], md["gs"]
    with k.phase():
        winb = k.sb("winb", [128, 8, M2_IN + 96], BF16)
        k.V(lambda e: e.memset(winb[:, :, M2_IN:M2_IN + 96], 0.0), w=winb.b)
        load_w_bf16(k, winb, w_in, 8, M2_IN, piece=1)
        cw = k.sb("cw", [128, 24, 4], F32); cb = k.sb("cb", [128, 24], F32); dtb = k.sb("dtb", [32, 1], F32)
        halo = k.sb("halo", [128, 24, 3], F32, nb=24)
        k.DMA("sync", cw[:], cw_d, w=cw.b); k.DMA("sync", cb[:], cb_d, w=cb.b); k.DMA("sync", dtb[:], dtb_d, w=dtb.b)
        k.DMA("sync", halo[:], halo_in, w=halo.b)
        xt = k.sb("xt", [128, 8, TT], F32)
        h = k.sb("h", [128, 8, TT], BF16)
        zst = [k.sb("zst", [128, 4, TT], BF16) for _ in range(2)]
        xst = [k.sb("xst", [128, 4, TT], BF16) for _ in range(2)]
        raw = [k.sb("raw", [128, TT + 3], F32) for _ in range(2)]
        cv = [k.sb("cv", [128, TT], F32) for _ in range(2)]
        e32 = k.sb("e32", [32, TT], F32)
        xs_v = x_src.rearrange("(k p) t -> p k t", p=128)
        sz_v = sz_d.rearrange("(c p) t -> p c t", p=128)
        xbc_v = xbc_d.rearrange("(c p) t -> p c t", p=128)
        for ti in range(NT // TT):
            t0 = ti * TT
            k.DMA("sync", xt[:], xs_v[:, :, t0:t0 + TT], w=xt.b)
            cm.norm_mod(xt, TT, gs, 0, mod, 0, h)
            for m in range(16):
                pb = cm.bank()
                for kk in range(8):
                    k.PE(lambda e, pb=pb, m=m, kk=kk: e.matmul(pb[:, :TT], lhsT=winb[:, kk, m * 128:(m + 1) * 128], rhs=h[:, kk, :],
                                                               start=(kk == 0), stop=(kk == 7)), r=winb.b + h.b, w=pb.b)
                zs = zst[(m // 4) % 2]
                k.A(lambda e, pb=pb, zs=zs, m=m: e.activation(out=zs[:, m % 4, :], in_=pb[:, :TT], func=AF.Silu), r=pb.b, w=zs.b)
                if m % 4 == 3:
                    k.DMA("sync", sz_v[:, m - 3:m + 1, t0:t0 + TT], zs[:], r=zs.b)
            for c in range(24):
                pb = cm.bank()
                c0 = 2048 + c * 128
                for kk in range(8):
                    k.PE(lambda e, pb=pb, c0=c0, kk=kk: e.matmul(pb[:, :TT], lhsT=winb[:, kk, c0:c0 + 128], rhs=h[:, kk, :],
                                                                 start=(kk == 0), stop=(kk == 7)), r=winb.b + h.b, w=pb.b)
                rw = raw[c % 2]; cv_ = cv[c % 2]; hb = halo.b[c:c + 1]
                k.A(lambda e, pb=pb, rw=rw: e.activation(out=rw[:, 3:3 + TT], in_=pb[:, :TT], func=AF.Identity), r=pb.b, w=rw.b)
                k.G(lambda e, rw=rw, c=c: e.tensor_copy(out=rw[:, 0:3], in_=halo[:, c, :]), r=hb, w=rw.b)
                k.A(lambda e, rw=rw, cv_=cv_, c=c: e.activation(out=cv_[:], in_=rw[:, 0:TT], func=AF.Identity, scale=cw[:, c, 0:1], bias=cb[:, c:c + 1]),
                    r=rw.b + cw.b + cb.b, w=cv_.b)
                for j in range(1, 4):
                    k.V(lambda e, rw=rw, cv_=cv_, c=c, j=j: e.scalar_tensor_tensor(out=cv_[:], in0=rw[:, j:j + TT], scalar=cw[:, c, j:j + 1], in1=cv_[:],
                                                                                   op0=ALU.mult, op1=ALU.add), r=rw.b + cw.b + cv_.b, w=cv_.b)
                k.G(lambda e, rw=rw, c=c: e.tensor_copy(out=halo[:, c, :], in_=rw[:, TT:TT + 3]), r=rw.b, w=hb)
                xs_ = xst[(c // 4) % 2]
                k.A(lambda e, cv_=cv_, xs_=xs_, c=c: e.activation(out=xs_[:, c % 4, :], in_=cv_[:], func=AF.Silu), r=cv_.b, w=xs_.b)
                if c % 4 == 3:
                    k.DMA("sync", xbc_v[:, c - 3:c + 1, t0:t0 + TT], xs_[:], r=xs_.b)
            pb = cm.bank()
            for kk in range(8):
                k.PE(lambda e, pb=pb, kk=kk: e.matmul(pb[:, :TT], lhsT=winb[:, kk, 5120:5248], rhs=h[:, kk, :],
                                                      start=(kk == 0), stop=(kk == 7)), r=winb.b + h.b, w=pb.b)
            k.A(lambda e, pb=pb: e.activation(out=e32[:], in_=pb[0:32, :TT], func=AF.Exp, bias=dtb[:, 0:1], scale=1.0), r=pb.b + dtb.b, w=e32.b)
            k.A(lambda e, t0=t0: e.activation(out=dt_fm[0:32, t0:t0 + TT], in_=e32[:], func=AF.Ln, bias=1.0, scale=1.0), r=e32.b, w=dt_fm.b)
        k.DMA("sync", halo_out, halo[:], r=halo.b)


def phase_ssd_b(k: K, cm: Common, pre, NT, md, x_src, x_dst, sz_d, xbc_d, dt_fm, st_in, st_out, TT=512):
    din = lambda n, shp: k.dram_in(f"{pre}{n}", shp)
    w_out = din("m2_w_out", [2 * D, D])
    alog_d = din("m2_alog", [128, 32]); drow_d = din("m2_drow", [128, 32]); ng_d = din("m2_ng", [128, 16])
    mod = md["mod"]
    U, ident, identb = cm.U, cm.ident, cm.identb
    with k.phase():
        woutb = k.sb("woutb", [128, 16, D], BF16)
        load_w_bf16(k, woutb, w_out, 16, D)
        arow = k.sb("arow", [128, 32], F32); drow = k.sb("drow", [128, 32], F32); ng = k.sb("ng", [128, 16], F32)
        k.DMA("sync", arow[:], alog_d, w=arow.b); k.DMA("sync", drow[:], drow_d, w=drow.b); k.DMA("sync", ng[:], ng_d, w=ng.b)
        k.A(lambda e: e.activation(out=arow[:], in_=arow[:], func=AF.Exp), r=arow.b, w=arow.b)
        k.V(lambda e: e.tensor_scalar(out=arow[:], in0=arow[:], scalar1=-1.0, scalar2=0.0, op0=ALU.mult, op1=ALU.add), r=arow.b, w=arow.b)
        DI = k.sb("DI", [128, 32, 128], BF16)
        for hd in range(32):
            k.V(lambda e, hd=hd: e.tensor_scalar(out=DI[:, hd, :], in0=ident[:], scalar1=drow[:, hd:hd + 1], scalar2=0.0, op0=ALU.mult, op1=ALU.add),
                r=ident.b + drow.b, w=DI.b)
        prev = k.sb("prev", [128, 32, 64], F32, nb=4)
        prevb = [k.sb("prevb", [128, 32, 64], BF16, nb=4) for _ in range(2)]
        k.DMA("sync", prev[:], st_in, w=prev.b)
        for par in range(2):
            k.V(lambda e, par=par: e.memset(prevb[par][:], 0.0), w=prevb[par].b)
            k.A(lambda e, par=par: e.activation(out=prevb[par][:].rearrange("p (a b) c -> p a b c", b=2)[:, :, par, :],
                                                in_=prev[:].rearrange("p (a b) c -> p a b c", b=2)[:, :, par, :], func=AF.Identity),
                r=prev.b, w=prevb[par].b)
        xt = k.sb("xt", [128, 8, TT], F32)
        xa = k.sb("xa", [128, 24, TT], BF16)
        sz = k.sb("sz", [128, 16, TT], BF16)
        y32 = k.sb("y32", [128, 16, TT], F32)
        xs_m = [k.sb("xs_m", [128, 2048], BF16) for _ in range(2)]
        for par in range(2):
            k.V(lambda e, par=par: e.memset(xs_m[par][:], 0.0), w=xs_m[par].b)
        B_tok = k.sb("B_tok", [128, 512], BF16)
        dt_tok = k.sb("dt_tok", [128, 32], F32); adt = k.sb("adt", [128, 32], F32)
        acs_tok = k.sb("acs_tok", [128, 32], F32); negacs = k.sb("negacs", [128, 32], F32)
        AU = k.sb("AU", [128, 8, 128], F32)
        seg = k.sb("seg", [128, 8, 128], F32); Lt = k.sb("Lt", [128, 8, 128], F32); Ebc = k.sb("Ebc", [128, 8, 128], F32)
        Gt = k.sb("Gt", [128, 8, 128], BF16); Ch = k.sb("Ch", [128, 8, 128], BF16)
        CBm = k.sb("CBm", [128, 128], F32)
        d1 = k.sb("d1", [128, 8], F32); dec = k.sb("dec", [128, 8], F32); scl = k.sb("scl", [128, 8], F32)
        xdtd = k.sb("xdtd", [128, 8, 64], BF16)
        xs_v = x_src.rearrange("(k p) t -> p k t", p=128)
        xd_v = x_dst.rearrange("(k p) t -> p k t", p=128)
        sz_v = sz_d.rearrange("(c p) t -> p c t", p=128)
        xbc_v = xbc_d.rearrange("(c p) t -> p c t", p=128)
        for ti in range(NT // TT):
            t0 = ti * TT
            k.DMA("sync", xt[:], xs_v[:, :, t0:t0 + TT], w=xt.b)
            k.DMA("sync", xa[:], xbc_v[:, :, t0:t0 + TT], w=xa.b)
            k.DMA("sync", sz[:], sz_v[:, :, t0:t0 + TT], w=sz.b)
            for c in range(TT // 128):
                cs = slice(c * 128, (c + 1) * 128)
                tg = t0 + c * 128
                pt = cm.bank()
                k.PE(lambda e, pt=pt, tg=tg: e.transpose(pt[:, 0:128], dt_fm[:, tg:tg + 128], ident[:]), r=dt_fm.b + ident.b, w=pt.b)
                k.A(lambda e, pt=pt: e.activation(out=dt_tok[:], in_=pt[:, 0:32], func=AF.Identity), r=pt.b, w=dt_tok.b)
                k.V(lambda e: e.tensor_tensor(out=adt[:], in0=dt_tok[:], in1=arow[:], op=ALU.mult), r=dt_tok.b + arow.b, w=adt.b)
                for half in range(2):
                    pbk = cm.bank()
                    ptb = pbk[:].bitcast(BF16)
                    for mm in range(8):
                        m = half * 8 + mm
                        k.PE(lambda e, ptb=ptb, mm=mm, m=m, cs=cs: e.transpose(ptb[:, mm * 128:(mm + 1) * 128], xa[:, m, cs], identb[:]),
                             r=xa.b + identb.b, w=pbk.b)
                    for par in range(2):
                        k.A(lambda e, ptb=ptb, half=half, par=par: e.activation(
                            out=xs_m[par][:, half * 1024:(half + 1) * 1024].rearrange("p (a b c) -> p a b c", b=2, c=64)[:, :, par, :],
                            in_=ptb[:, 0:1024].rearrange("p (a b c) -> p a b c", b=2, c=64)[:, :, par, :], func=AF.Identity),
                            r=pbk.b, w=xs_m[par].b)
                pbk = cm.bank()
                ptb2 = pbk[:].bitcast(BF16)
                for g in range(4):
                    k.PE(lambda e, ptb2=ptb2, g=g, cs=cs: e.transpose(ptb2[:, g * 128:(g + 1) * 128], xa[:, 16 + g, cs], identb[:]),
                         r=xa.b + identb.b, w=pbk.b)
                k.V(lambda e, ptb2=ptb2: e.tensor_copy(out=B_tok[:], in_=ptb2[:, 0:512]), r=pbk.b, w=B_tok.b)
                pa = cm.bank()
                k.PE(lambda e, pa=pa: e.matmul(pa[:, 0:32], lhsT=U[:], rhs=adt[:], start=True, stop=True), r=U.b + adt.b, w=pa.b)
                k.A(lambda e, pa=pa: e.activation(out=acs_tok[:], in_=pa[:, 0:32], func=AF.Identity), r=pa.b, w=acs_tok.b)
                k.V(lambda e: e.tensor_scalar(out=negacs[:], in0=acs_tok[:], scalar1=-1.0, scalar2=0.0, op0=ALU.mult, op1=ALU.add), r=acs_tok.b, w=negacs.b)
                for g in range(4):
                    h0 = 8 * g
                    k.V(lambda e, h0=h0: e.tensor_tensor(out=AU[:], in0=U[:].unsqueeze(1).to_broadcast([128, 8, 128]),
                                                         in1=adt[:, h0:h0 + 8].unsqueeze(2).to_broadcast([128, 8, 128]), op=ALU.mult),
                        r=U.b + adt.b, w=AU.b)
                    pacs = [cm.bank(), cm.bank()]
                    for hf in range(2):
                        k.PE(lambda e, hf=hf, pacs=pacs: e.matmul(pacs[hf][:, 0:512], lhsT=cm.ones32[:],
                                                                  rhs=AU[:, 4 * hf:4 * hf + 4, :].rearrange("p a b -> p (a b)"), start=True, stop=True),
                             r=cm.ones32.b + AU.b, w=pacs[hf].b)
                    pcb = cm.bank()
                    k.PE(lambda e, pcb=pcb, g=g, cs=cs: e.matmul(pcb[:, 0:128], lhsT=xa[:, 16 + g, cs], rhs=xa[:, 20 + g, cs], start=True, stop=True),
                         r=xa.b, w=pcb.b)
                    k.V(lambda e, pcb=pcb: e.tensor_tensor(out=CBm[:], in0=pcb[:, 0:128], in1=U[:], op=ALU.mult), r=pcb.b + U.b, w=CBm.b)
                    for hf in range(2):
                        k.A(lambda e, hf=hf, pacs=pacs: e.activation(out=Ebc[:, 4 * hf:4 * hf + 4, :].rearrange("p a b -> p (a b)"), in_=pacs[hf][:, 0:512], func=AF.Exp),
                            r=pacs[hf].b, w=Ebc.b)
                    for hh in range(8):
                        hd = h0 + hh
                        pp = pacs[hh // 4]
                        k.V(lambda e, pp=pp, hh=hh, hd=hd: e.tensor_scalar(out=seg[:, hh, :], in0=pp[:, (hh % 4) * 128:(hh % 4 + 1) * 128],
                                                                           scalar1=negacs[:, hd:hd + 1], scalar2=0.0, op0=ALU.add, op1=ALU.min),
                            r=pp.b + negacs.b, w=seg.b)
                    k.A(lambda e: e.activation(out=Lt[:].rearrange("p a b -> p (a b)"), in_=seg[:].rearrange("p a b -> p (a b)"), func=AF.Exp), r=seg.b, w=Lt.b)
                    for hh in range(8):
                        hd = h0 + hh
                        k.V(lambda e, hh=hh, hd=hd: e.scalar_tensor_tensor(out=Gt[:, hh, :], in0=Lt[:, hh, :], scalar=dt_tok[:, hd:hd + 1], in1=CBm[:],
                                                                           op0=ALU.mult, op1=ALU.mult), r=Lt.b + dt_tok.b + CBm.b, w=Gt.b)
                    k.V(lambda e, g=g, cs=cs: e.tensor_tensor(out=Ch[:], in0=Ebc[:], in1=xa[:, 20 + g, cs].unsqueeze(1).to_broadcast([128, 8, 128]), op=ALU.mult),
                        r=Ebc.b + xa.b, w=Ch.b)
                    for hf in range(2):
                        k.V(lambda e, hf=hf, pacs=pacs, h0=h0: e.tensor_tensor(
                            out=d1[:, 4 * hf:4 * hf + 4], in0=pacs[hf][:, 0:512].rearrange("p (a b) -> p a b", a=4)[:, :, 127],
                            in1=acs_tok[:, h0 + 4 * hf:h0 + 4 * hf + 4], op=ALU.subtract), r=pacs[hf].b + acs_tok.b, w=d1.b)
                    k.A(lambda e: e.activation(out=dec[:], in_=d1[:], func=AF.Exp), r=d1.b, w=dec.b)
                    k.V(lambda e, h0=h0: e.tensor_tensor(out=scl[:], in0=dec[:], in1=dt_tok[:, h0:h0 + 8], op=ALU.mult), r=dec.b + dt_tok.b, w=scl.b)
                    for par in range(2):
                        k.V(lambda e, g=g, par=par: e.tensor_tensor(
                            out=xdtd[:].rearrange("p (a b) c -> p a b c", b=2)[:, :, par, :],
                            in0=xs_m[par][:, g * 512:(g + 1) * 512].rearrange("p (a b c) -> p a b c", b=2, c=64)[:, :, par, :],
                            in1=scl[:].rearrange("p (a b) -> p a b", b=2)[:, :, par].unsqueeze(2).to_broadcast([128, 4, 64]), op=ALU.mult),
                            r=xs_m[par].b + scl.b, w=xdtd.b)
                    py = cm.bank()
                    pvb = prevb[0].b[g:g + 1] + prevb[1].b[g:g + 1]
                    for hp in range(4):
                        pr0 = (h0 + 2 * hp) * 64
                        o_ = lambda py=py, hp=hp: py[:, hp * 128:(hp + 1) * 128]
                        n = 0
                        for e_ in range(2):
                            hh = 2 * hp + e_
                            hd = h0 + hh
                            for rhs_fn, lw, rb in ((lambda hh=hh: Gt[:, hh, :], xs_m[e_], Gt.b), (lambda hd=hd: DI[:, hd, :], xs_m[e_], DI.b)):
                                k.PE(lambda e, o_=o_, lw=lw, pr0=pr0, rhs_fn=rhs_fn, n=n: e.matmul(o_(), lhsT=lw[:, pr0:pr0 + 128], rhs=rhs_fn(),
                                                                                                  start=(n == 0), stop=False), r=lw.b + rb, w=py.b)
                                n += 1
                        for e_ in range(2):
                            hh = 2 * hp + e_
                            hp_g = (h0 + 2 * hp)
                            k.PE(lambda e, o_=o_, e_=e_, hp_g=hp_g, hh=hh: e.matmul(
                                o_(), lhsT=prevb[e_][:, hp_g:hp_g + 2, :].rearrange("p a b -> p (a b)"), rhs=Ch[:, hh, :], start=False, stop=(e_ == 1)),
                                r=pvb + Ch.b, w=py.b)
                    k.A(lambda e, py=py, g=g, cs=cs: e.activation(out=y32[:, 4 * g:4 * g + 4, cs], in_=py[:, 0:512].rearrange("p (a b) -> p a b", a=4), func=AF.Identity),
                        r=py.b, w=y32.b)
                    pst = cm.bank()
                    k.PE(lambda e, pst=pst, g=g: e.matmul(pst[:, 0:512], lhsT=B_tok[:, g * 128:(g + 1) * 128], rhs=xdtd[:].rearrange("p a b -> p (a b)"),
                                                          start=True, stop=True), r=B_tok.b + xdtd.b, w=pst.b)
                    pv = prev.b[g:g + 1]
                    for hh in range(8):
                        hd = h0 + hh
                        k.V(lambda e, pst=pst, hh=hh, hd=hd: e.scalar_tensor_tensor(out=prev[:, hd, :], in0=prev[:, hd, :], scalar=Ebc[:, hh, 127:128],
                                                                                    in1=pst[:, hh * 64:(hh + 1) * 64], op0=ALU.mult, op1=ALU.add),
                            r=pv + Ebc.b + pst.b, w=pv)
                    for par in range(2):
                        k.A(lambda e, h0=h0, par=par: e.activation(
                            out=prevb[par][:, h0:h0 + 8, :].rearrange("p (a b) c -> p a b c", b=2)[:, :, par, :],
                            in_=prev[:, h0:h0 + 8, :].rearrange("p (a b) c -> p a b c", b=2)[:, :, par, :], func=AF.Identity),
                            r=pv, w=prevb[par].b[g:g + 1])
            for m in range(16):
                k.V(lambda e, m=m: e.tensor_tensor(out=y32[:, m, :], in0=y32[:, m, :], in1=sz[:, m, :], op=ALU.mult), r=y32.b + sz.b, w=y32.b)
            for g in range(4):
                rstd = cm.stats_rstd([y32[:, 4 * g + j, :] for j in range(4)], [y32.b] * 4, TT, cm.ones512)
                for j in range(4):
                    m = 4 * g + j
                    k.V(lambda e, m=m: e.scalar_tensor_tensor(out=xa[:, m, :], in0=y32[:, m, :], scalar=ng[:, m:m + 1], in1=rstd[:, :TT],
                                                              op0=ALU.mult, op1=ALU.mult), r=y32.b + ng.b + rstd.b, w=xa.b)
            for m in range(8):
                pb = cm.bank()
                for kk in range(16):
                    k.PE(lambda e, pb=pb, m=m, kk=kk: e.matmul(pb[:, :TT], lhsT=woutb[:, kk, m * 128:(m + 1) * 128], rhs=xa[:, kk, :],
                                                               start=(kk == 0), stop=(kk == 15)), r=woutb.b + xa.b, w=pb.b)
                k.V(lambda e, pb=pb, m=m: e.scalar_tensor_tensor(out=xt[:, m, :], in0=pb[:, :TT], scalar=mod[:, 16 + m:17 + m], in1=xt[:, m, :],
                                                                 op0=ALU.mult, op1=ALU.add), r=pb.b + mod.b + xt.b, w=xt.b)
            k.DMA("sync", xd_v[:, :, t0:t0 + TT], xt[:], r=xt.b)
        k.DMA("sync", st_out, prev[:], r=prev.b)


def layer_ssd(k: K, cm: Common, li, pre, NT, md, x_src, x_mid, x_dst, scr, halo_in, halo_out, st_in, st_out, final_g=None):
    outer = k.es
    with ExitStack() as les:
        k.es = les
        dt_fm = k.sb("dt_fm", [128, NT], F32)
        k.V(lambda e: e.memset(dt_fm[:], 0.0), w=dt_fm.b)
        sz_d, xbc_d = scr
        phase_ssd_a(k, cm, pre, NT, md, x_src, sz_d, xbc_d, dt_fm, halo_in, halo_out)
        phase_ssd_b(k, cm, pre, NT, md, x_src, x_mid, sz_d, xbc_d, dt_fm, st_in, st_out)
        k.es = outer
    phase_ffn(k, cm, li, NT, md, x_mid, x_dst, final_g=final_g)


def layer_s5(k: K, cm: Common, li, pre, NT, md, x_src, x_mid, x_dst, st_in, st_out, final_g=None):
    phase_s5(k, cm, pre, NT, md, x_src, x_mid, st_in, st_out)
    phase_ffn(k, cm, li, NT, md, x_mid, x_dst, final_g=final_g)


DEPTH = 4
SEQ = 16384
SEG = 4096
N_CORES = 2


def build_full(seq=SEQ, seg=SEG, depth=DEPTH, carry_io=False):
    nc = bass.Bass("TRN2", target_bir_lowering=False)
    with ExitStack() as es:
        k = K(nc, es)
        cm = Common(k)
        xin = k.dram_in("xT", [D, seq])
        xout = k.dram_out("outT", [D, seq])
        xwork = k.dram_tmp("xwork", [D, seq], F32)
        scr = (k.dram_tmp("sz_scr", [2 * D, seg], BF16), k.dram_tmp("xbc_scr", [3 * D, seg], BF16))
        if not carry_io:
            s5_z = k.dram_in("s5_st0", [128, 32, 2])
            ssd_z = k.dram_in("ssd_st0", [128, 32, 64])
            halo_z = k.dram_in("halo0", [128, 24, 3])
        nseg = seq // seg
        for li in range(depth):
            md = cm.adaln(li)
            src = xin if li == 0 else xwork
            dst = xout if li == depth - 1 else xwork
            fin = True if li == depth - 1 else None
            st_prev = None
            halo_prev = None
            pre = f"L{li}_"
            if carry_io:
                if li % 2 == 0:
                    st_first = k.dram_in(f"{pre}st_in", [128, 32, 2])
                    st_last = k.dram_out(f"{pre}st_out", [128, 32, 2])
                else:
                    st_first = k.dram_in(f"{pre}st_in", [128, 32, 64])
                    st_last = k.dram_out(f"{pre}st_out", [128, 32, 64])
                    halo_first = k.dram_in(f"{pre}halo_in", [128, 24, 3])
                    halo_last = k.dram_out(f"{pre}halo_out", [128, 24, 3])
            else:
                st_first = s5_z if li % 2 == 0 else ssd_z
                halo_first = halo_z
                st_last = halo_last = None
            for sg in range(nseg):
                xs = src[:, sg * seg:(sg + 1) * seg]
                xm = xwork[:, sg * seg:(sg + 1) * seg]
                xd = dst[:, sg * seg:(sg + 1) * seg]
                last = (sg == nseg - 1)
                if li % 2 == 0:
                    st_out = st_last if (last and st_last is not None) else k.dram_tmp("s5st", [128, 32, 2], F32)
                    layer_s5(k, cm, li, pre, seg, md, xs, xm, xd, st_first if sg == 0 else st_prev, st_out, final_g=fin)
                else:
                    st_out = st_last if (last and st_last is not None) else k.dram_tmp("ssdst", [128, 32, 64], F32)
                    halo_out = halo_last if (last and halo_last is not None) else k.dram_tmp("halo", [128, 24, 3], F32)
                    layer_ssd(k, cm, li, pre, seg, md, xs, xm, xd, scr, halo_first if sg == 0 else halo_prev, halo_out,
                              st_first if sg == 0 else st_prev, st_out, final_g=fin)
                    halo_prev = halo_out
                st_prev = st_out
        k.P.finish()
        k.P.emit()
    return nc


def host_inputs(inputs, b, depth=DEPTH):
    f32 = np.float32
    d = dict(const_inputs())
    d["xT"] = np.ascontiguousarray(np.asarray(inputs["x"][b], f32).T)
    d["c_col"] = col(inputs["c"][b])
    d["s5_st0"] = np.zeros((128, 32, 2), f32)
    d["ssd_st0"] = np.zeros((128, 32, 64), f32)
    d["halo0"] = np.zeros((128, 24, 3), f32)
    d["gfinal"] = col(inputs["final_norm_g"])
    for li in range(depth):
        d[f"ada_w{li}"] = np.asarray(inputs["ada_w"][li], f32)
        d[f"ada_b{li}"] = col(inputs["ada_b"][li])
        d[f"gmix{li}"] = col(inputs["norm_mix_g"][li])
        d[f"gmlp{li}"] = col(inputs["norm_mlp_g"][li])
        d[f"mlp_w1_{li}"] = np.asarray(inputs["mlp_w1"][li], f32)
        d[f"mlp_w2_{li}"] = np.asarray(inputs["mlp_w2"][li], f32)
        if li % 2 == 0:
            d.update(s5_host_inputs(li // 2, inputs, f"L{li}_"))
        else:
            d.update(ssd_host_inputs(li // 2, inputs, f"L{li}_"))
    return d


CHUNK = 4096


def kernel(**inputs):
    inputs = {kk: np.asarray(v) for kk, v in inputs.items()}
    nb, L = inputs["x"].shape[0], inputs["x"].shape[1]
    f32 = np.float32
    base = [host_inputs(inputs, b) for b in range(nb)]
    for d in base:
        for kk in ("xT", "s5_st0", "ssd_st0", "halo0"):
            d.pop(kk)
    carry = []
    for b in range(nb):
        c = {}
        for li in range(DEPTH):
            if li % 2 == 0:
                c[f"L{li}_st_in"] = np.zeros((128, 32, 2), f32)
            else:
                c[f"L{li}_st_in"] = np.zeros((128, 32, 64), f32)
                c[f"L{li}_halo_in"] = np.zeros((128, 24, 3), f32)
        carry.append(c)
    out = np.zeros((nb, L, D), f32)
    nc = build_full(seq=CHUNK, seg=min(SEG, CHUNK), carry_io=True)
    for t0 in range(0, L, CHUNK):
        in_maps = []
        for b in range(nb):
            d = dict(base[b])
            d.update(carry[b])
            d["xT"] = np.ascontiguousarray(np.asarray(inputs["x"][b, t0:t0 + CHUNK], f32).T)
            in_maps.append(d)
        res = run_bass_kernel_spmd(nc, in_maps, core_ids=list(range(nb)))
        for b in range(nb):
            r = res.results[b]
            out[b, t0:t0 + CHUNK] = r["outT"].T
            for li in range(DEPTH):
                carry[b][f"L{li}_st_in"] = np.asarray(r[f"L{li}_st_out"], f32)
                if li % 2 == 1:
                    carry[b][f"L{li}_halo_in"] = np.asarray(r[f"L{li}_halo_out"], f32)
    return out
```
